# Optimizing a Trainium2 kernel written in Bass

```python
import jax
import jax.numpy as jnp
from jax import lax
import numpy as np

D_MODEL = 2048
BATCH = 2
SEQ = 8192
DEPTH = 2

GRID_W = 64
CTX_LEN = 256
N_MOD = 9
D_FF = 5632
EPS = 1e-6
ROPE_BASE = 10000.0
BLOCK = 128

A_HEADS = 8
A_KV_HEADS = 2
A_REP = A_HEADS // A_KV_HEADS
A_HEAD_DIM = 128
A_WINDOW = 128

B_HEADS = 8
B_NOPE = 128
B_ROPE = 64
B_QK = B_NOPE + B_ROPE
B_V = 128
B_Q_RANK = 768
B_KV_RANK = 256

A_Q_W = A_HEADS * A_HEAD_DIM
A_KV_W = A_KV_HEADS * A_HEAD_DIM
IN_SPLITS = (A_Q_W,
             A_Q_W + A_KV_W,
             A_Q_W + 2 * A_KV_W,
             A_Q_W + 2 * A_KV_W + B_Q_RANK,
             A_Q_W + 2 * A_KV_W + B_Q_RANK + B_KV_RANK)
IN_W = IN_SPLITS[-1] + B_ROPE
MIX_W = A_HEADS * A_HEAD_DIM + B_HEADS * B_V

POOL_WINDOWS = (2, 4, 8, 16)
POOL_GROUPS = len(POOL_WINDOWS)
POOL_GW = D_MODEL // POOL_GROUPS

N_ATTN_LAYERS = (DEPTH + 1) // 2
N_POOL_LAYERS = DEPTH // 2

kernel_name = 'hybrid_swa_mla_pool_macaron_dit'


def rmsnorm(x, g):
    xf = x.astype(jnp.float32)
    y = xf * lax.rsqrt(jnp.mean(xf * xf, axis=-1, keepdims=True) + EPS)
    return (y * g.astype(jnp.float32)).astype(x.dtype)


def modulate(h, shift, scale):
    return h * (1.0 + scale) + shift


def swiglu(h, w_gate, w_up, w_down):
    return (jax.nn.silu(h @ w_gate) * (h @ w_up)) @ w_down


def axial_rope(n, rot_dim, dtype):
    rows = n // GRID_W
    row = jnp.repeat(jnp.arange(rows), GRID_W).astype(jnp.float32)
    col = jnp.tile(jnp.arange(GRID_W), rows).astype(jnp.float32)
    nf = rot_dim // 4
    inv = ROPE_BASE ** (-jnp.arange(nf, dtype=jnp.float32) / nf)
    ang = jnp.concatenate([row[:, None] * inv, col[:, None] * inv], axis=-1)
    return jnp.cos(ang)[:, None, :].astype(dtype), jnp.sin(ang)[:, None, :].astype(dtype)


def apply_rope(x, cos, sin):
    half = x.shape[-1] // 2
    x1, x2 = x[..., :half], x[..., half:]
    return jnp.concatenate([x1 * cos - x2 * sin, x1 * sin + x2 * cos], axis=-1)


def window_gqa(q, k, v, kc, vc, sink):
    b, n = q.shape[:2]
    nc = kc.shape[1]
    nb = n // BLOCK
    scale = A_HEAD_DIM ** -0.5
    qb = q.reshape(b, nb, BLOCK, A_KV_HEADS, A_REP, A_HEAD_DIM)

    def band(t):
        tp = jnp.pad(t, ((0, 0), (BLOCK, BLOCK), (0, 0), (0, 0)))
        tp = tp.reshape(b, nb + 2, BLOCK, A_KV_HEADS, A_HEAD_DIM)
        return jnp.concatenate([tp[:, :-2], tp[:, 1:-1], tp[:, 2:]], axis=2)

    kb, vb = band(k), band(v)
    s_loc = jnp.einsum('bnqgrd,bnkgd->bngrqk', qb, kb).astype(jnp.float32) * scale
    blk = jnp.arange(nb)[:, None, None]
    qpos = blk * BLOCK + jnp.arange(BLOCK)[None, :, None]
    kpos = (blk - 1) * BLOCK + jnp.arange(3 * BLOCK)[None, None, :]
    valid = (jnp.abs(kpos - qpos) <= A_WINDOW) & (kpos >= 0) & (kpos < n)
    s_loc = jnp.where(valid[None, :, None, None], s_loc, -jnp.inf)
    s_ctx = jnp.einsum('bnqgrd,bcgd->bngrqc', qb, kc).astype(jnp.float32) * scale
    s_sink = jnp.broadcast_to(sink.astype(jnp.float32).reshape(1, 1, A_KV_HEADS, A_REP, 1, 1),
                              s_loc.shape[:-1] + (1,))
    p = jax.nn.softmax(jnp.concatenate([s_loc, s_ctx, s_sink], axis=-1), axis=-1).astype(v.dtype)
    kl = 3 * BLOCK
    o = (jnp.einsum('bngrqk,bnkgd->bnqgrd', p[..., :kl], vb)
         + jnp.einsum('bngrqc,bcgd->bnqgrd', p[..., kl:kl + nc], vc))
    return o.reshape(b, n, A_HEADS * A_HEAD_DIM)


def context_gqa(qc, kc, vc, sink):
    b, nc = qc.shape[:2]
    scale = A_HEAD_DIM ** -0.5
    qg = qc.reshape(b, nc, A_KV_HEADS, A_REP, A_HEAD_DIM)
    s = jnp.einsum('bqgrd,bkgd->bgrqk', qg, kc).astype(jnp.float32) * scale
    s_sink = jnp.broadcast_to(sink.astype(jnp.float32).reshape(1, A_KV_HEADS, A_REP, 1, 1),
                              s.shape[:-1] + (1,))
    p = jax.nn.softmax(jnp.concatenate([s, s_sink], axis=-1), axis=-1)[..., :nc].astype(vc.dtype)
    return jnp.einsum('bgrqk,bkgd->bqgrd', p, vc).reshape(b, nc, A_HEADS * A_HEAD_DIM)


def mla_keys(ckv, kr, kv_norm, w_ukv, rope):
    b, n = ckv.shape[:2]
    kv = (rmsnorm(ckv, kv_norm) @ w_ukv).reshape(b, n, B_HEADS, B_NOPE + B_V)
    kr = kr[:, :, None, :]
    if rope is not None:
        kr = apply_rope(kr, *rope)
    k = jnp.concatenate([kv[..., :B_NOPE], jnp.broadcast_to(kr, (b, n, B_HEADS, B_ROPE))], axis=-1)
    return k, kv[..., B_NOPE:]


def mla_queries(cq, q_norm, w_uq, rope):
    b, n = cq.shape[:2]
    q = (rmsnorm(cq, q_norm) @ w_uq).reshape(b, n, B_HEADS, B_QK)
    if rope is None:
        return q
    return jnp.concatenate([q[..., :B_NOPE], apply_rope(q[..., B_NOPE:], *rope)], axis=-1)


def block_dense_attention(q, k, v):
    b, n = q.shape[:2]
    nb = n // BLOCK
    scale = B_QK ** -0.5
    qb = jnp.moveaxis(q.reshape(b, nb, BLOCK, B_HEADS, B_QK), 1, 0)

    def one_block(qblk):
        s = jnp.einsum('bqhd,bkhd->bhqk', qblk, k).astype(jnp.float32) * scale
        p = jax.nn.softmax(s, axis=-1).astype(v.dtype)
        return jnp.einsum('bhqk,bkhd->bqhd', p, v)

    o = lax.map(one_block, qb)
    return jnp.moveaxis(o, 0, 1).reshape(b, n, B_HEADS * B_V)


def attention_mixer(h, hc, ctx_out, w_in, sink, q_norm, w_uq, kv_norm, w_ukv, w_out, rope_a, rope_b):
    b, n, _ = h.shape
    nc = hc.shape[1]
    aq, ak, av, bcq, bckv, bkr = jnp.split(h @ w_in, IN_SPLITS, axis=-1)
    caq, cak, cav, cbcq, cbckv, cbkr = jnp.split(hc @ w_in, IN_SPLITS, axis=-1)
    q_a = apply_rope(aq.reshape(b, n, A_HEADS, A_HEAD_DIM), *rope_a)
    k_a = apply_rope(ak.reshape(b, n, A_KV_HEADS, A_HEAD_DIM), *rope_a)
    v_a = av.reshape(b, n, A_KV_HEADS, A_HEAD_DIM)
    kc_a = cak.reshape(b, nc, A_KV_HEADS, A_HEAD_DIM)
    vc_a = cav.reshape(b, nc, A_KV_HEADS, A_HEAD_DIM)
    o_a = window_gqa(q_a, k_a, v_a, kc_a, vc_a, sink)
    k_b, v_b = mla_keys(bckv, bkr, kv_norm, w_ukv, rope_b)
    kc_b, vc_b = mla_keys(cbckv, cbkr, kv_norm, w_ukv, None)
    q_b = mla_queries(bcq, q_norm, w_uq, rope_b)
    o_b = block_dense_attention(q_b, jnp.concatenate([kc_b, k_b], axis=1),
                                jnp.concatenate([vc_b, v_b], axis=1))
    y = jnp.concatenate([o_a, o_b], axis=-1) @ w_out
    if not ctx_out:
        return y, None
    oc_a = context_gqa(caq.reshape(b, nc, A_HEADS, A_HEAD_DIM), kc_a, vc_a, sink)
    oc_b = block_dense_attention(mla_queries(cbcq, q_norm, w_uq, None), kc_b, vc_b)
    yc = jnp.concatenate([oc_a, oc_b], axis=-1) @ w_out
    return y, yc


def centred_window_mean(xg, w):
    n = xg.shape[1]
    cs = jnp.pad(jnp.cumsum(xg.astype(jnp.float32), axis=1), ((0, 0), (1, 0), (0, 0)))
    t = jnp.arange(n)
    lo = jnp.clip(t - w // 2, 0, n)
    hi = jnp.clip(t - w // 2 + w, 0, n)
    s = jnp.take(cs, hi, axis=1) - jnp.take(cs, lo, axis=1)
    return (s / (hi - lo).astype(jnp.float32)[None, :, None]).astype(xg.dtype)


def pool_mixer(h, w, scale):
    b, n, d = h.shape
    hg = h.reshape(b, n, POOL_GROUPS, POOL_GW)
    pooled = jnp.stack([centred_window_mean(hg[:, :, g], POOL_WINDOWS[g]) for g in range(POOL_GROUPS)],
                       axis=2)
    y = jnp.einsum('bngc,gcd->bngd', pooled - hg, w).reshape(b, n, d)
    return y * scale


def setup_inputs(seed: int = 0) -> dict:
    key = jax.random.key(seed)
    ks = jax.random.split(key, 32)
    cnt = [0]
    f32 = jnp.float32

    def nk():
        cnt[0] += 1
        return ks[cnt[0] - 1]

    def dense(shape, fan_in, gain=1.0):
        return jax.random.normal(nk(), shape, f32) * (gain * fan_in ** -0.5)

    def gainv(shape):
        return 1.0 + 0.05 * jax.random.normal(nk(), shape, f32)

    def small(shape, s):
        return s * jax.random.normal(nk(), shape, f32)

    D, F = D_MODEL, D_FF
    NA, NP = N_ATTN_LAYERS, N_POOL_LAYERS
    return {
        'x': jax.random.normal(nk(), (BATCH, SEQ, D), f32),
        'c': jax.random.normal(nk(), (BATCH, D), f32),
        'ctx': jax.random.normal(nk(), (BATCH, CTX_LEN, D), f32),
        'c_ctx': jax.random.normal(nk(), (D,), f32),
        'w_ada': dense((DEPTH, D, N_MOD * D), D, 0.5),
        'b_ada': small((DEPTH, N_MOD * D), 0.02),
        'norm_ffn1': gainv((DEPTH, D)),
        'norm_mix': gainv((DEPTH, D)),
        'norm_ffn2': gainv((DEPTH, D)),
        'ffn1_w_gate': dense((DEPTH, D, F), D),
        'ffn1_w_up': dense((DEPTH, D, F), D),
        'ffn1_w_down': dense((DEPTH, F, D), F),
        'ffn2_w_gate': dense((DEPTH, D, F), D),
        'ffn2_w_up': dense((DEPTH, D, F), D),
        'ffn2_w_down': dense((DEPTH, F, D), F),
        'attn_w_in': dense((NA, D, IN_W), D),
        'attn_sink': small((NA, A_HEADS), 0.5),
        'mla_q_norm': gainv((NA, B_Q_RANK)),
        'mla_w_uq': dense((NA, B_Q_RANK, B_HEADS * B_QK), B_Q_RANK),
        'mla_kv_norm': gainv((NA, B_KV_RANK)),
        'mla_w_ukv': dense((NA, B_KV_RANK, B_HEADS * (B_NOPE + B_V)), B_KV_RANK),
        'attn_w_out': dense((NA, MIX_W, D), MIX_W),
        'pool_w': dense((NP, POOL_GROUPS, POOL_GW, POOL_GW), POOL_GW),
        'pool_scale': gainv((NP, D)),
        'final_norm': gainv((D,)),
    }


def reference(x, c, ctx, c_ctx, w_ada, b_ada, norm_ffn1, norm_mix, norm_ffn2,
              ffn1_w_gate, ffn1_w_up, ffn1_w_down, ffn2_w_gate, ffn2_w_up, ffn2_w_down,
              attn_w_in, attn_sink, mla_q_norm, mla_w_uq, mla_kv_norm, mla_w_ukv, attn_w_out,
              pool_w, pool_scale, final_norm):
    n = x.shape[1]
    rope_a = axial_rope(n, A_HEAD_DIM, x.dtype)
    rope_b = axial_rope(n, B_ROPE, x.dtype)
    cx = ctx
    for l in range(DEPTH):
        is_attn = (l % 2 == 0)
        ctx_out = any(j % 2 == 0 for j in range(l + 1, DEPTH))
        ctx_in = is_attn or ctx_out
        m = jnp.split((jax.nn.silu(c) @ w_ada[l] + b_ada[l])[:, None, :], N_MOD, axis=-1)
        mc = jnp.split(jax.nn.silu(c_ctx) @ w_ada[l] + b_ada[l], N_MOD, axis=-1)
        x = x + 0.5 * m[2] * swiglu(modulate(rmsnorm(x, norm_ffn1[l]), m[0], m[1]),
                                    ffn1_w_gate[l], ffn1_w_up[l], ffn1_w_down[l])
        if ctx_in:
            cx = cx + 0.5 * mc[2] * swiglu(modulate(rmsnorm(cx, norm_ffn1[l]), mc[0], mc[1]),
                                           ffn1_w_gate[l], ffn1_w_up[l], ffn1_w_down[l])
        h = modulate(rmsnorm(x, norm_mix[l]), m[3], m[4])
        hc = modulate(rmsnorm(cx, norm_mix[l]), mc[3], mc[4]) if ctx_in else None
        i = l // 2
        if is_attn:
            y, yc = attention_mixer(h, hc, ctx_out, attn_w_in[i], attn_sink[i], mla_q_norm[i], mla_w_uq[i],
                                    mla_kv_norm[i], mla_w_ukv[i], attn_w_out[i], rope_a, rope_b)
        else:
            y = pool_mixer(h, pool_w[i], pool_scale[i])
            yc = pool_mixer(hc, pool_w[i], pool_scale[i]) if ctx_out else None
        x = x + m[5] * y
        if ctx_out:
            cx = cx + mc[5] * yc
        x = x + 0.5 * m[8] * swiglu(modulate(rmsnorm(x, norm_ffn2[l]), m[6], m[7]),
                                    ffn2_w_gate[l], ffn2_w_up[l], ffn2_w_down[l])
        if ctx_out:
            cx = cx + 0.5 * mc[8] * swiglu(modulate(rmsnorm(cx, norm_ffn2[l]), mc[6], mc[7]),
                                           ffn2_w_gate[l], ffn2_w_up[l], ffn2_w_down[l])
    return rmsnorm(x, final_norm)
```

```python
import numpy as np
from concourse.bass_utils import run_bass_kernel_spmd
from contextlib import ExitStack
import concourse.bass as bass
import concourse.mybir as mybir

F32 = mybir.dt.float32
BF16 = mybir.dt.bfloat16
AF = mybir.ActivationFunctionType
ALU = mybir.AluOpType


class Buf:
    __slots__ = ("wr", "rd", "dsem", "name")

    def __init__(self, name="", dsem=None):
        self.wr = None
        self.rd = {}
        self.dsem = dsem
        self.name = name


class DSem:
    def __init__(self, h):
        self.h = h
        self.cnt = 0


class Eng:
    def __init__(self, K, eng, name, compute=True):
        self.K = K
        self.eng = eng
        self.name = name
        self.compute = compute
        self.sem = K.sem("s_" + name) if compute else None
        self.cnt = 0
        self.seen = {}
        self.last = None

    def wait(self, ev):
        if ev is None:
            return
        h, val, key = ev
        if self.seen.get(key, 0) >= val:
            return
        self.eng.wait_ge(h, val)
        self.seen[key] = val

    def mark(self, ins):
        self.cnt += 1
        ins.then_inc(self.sem, 1)
        ev = (self.sem, self.cnt, "E" + self.name)
        self.seen["E" + self.name] = self.seen.get("E" + self.name, 0)
        self.last = ev
        return ev


class K:
    def __init__(self, nc):
        self.nc = nc
        self.es = ExitStack()
        self.nsem = 0
        self.pe = Eng(self, nc.tensor, "pe")
        self.act = Eng(self, nc.scalar, "act")
        self.dve = Eng(self, nc.vector, "dve")
        self.pool = Eng(self, nc.gpsimd, "pool")
        self.q_sync = Eng(self, nc.sync, "qsync", compute=False)
        self.q_act = self.act
        self.q_pool = self.pool
        self.dsems = []
        self.dcache = {}
        self.scopes = []

    def sem(self, name):
        self.nsem += 1
        return self.es.enter_context(self.nc.semaphore(name))

    def dsem(self, name):
        d = DSem(self.sem("d_" + name))
        d.key = "D" + name + str(len(self.dsems))
        self.dsems.append(d)
        return d

    def dbuf(self, name):
        if name not in self.dcache:
            self.dcache[name] = Buf(name, self.dsem(name))
        return self.dcache[name]

    def sbuf(self, name, shape, dt, es=None):
        return (es or self.es).enter_context(self.nc.sbuf_tensor(name, shape, dt))

    def psum(self, name, shape, dt, es=None):
        return (es or self.es).enter_context(self.nc.psum_tensor(name, shape, dt))

    def _pre(self, E, reads, writes):
        evs = {}

        def add(ev):
            if ev is None:
                return
            if E is self.pe and ev[2] == "Epe":
                return
            if ev[2] not in evs or evs[ev[2]][1] < ev[1]:
                evs[ev[2]] = ev
        for b in reads:
            add(b.wr)
        for b in writes:
            add(b.wr)
            for r in b.rd.values():
                add(r)
        for ev in evs.values():
            E.wait(ev)

    def _post(self, E, ev, reads, writes):
        for b in reads:
            b.rd[E.name] = ev
        for b in writes:
            b.wr = ev
            b.rd = {}

    def op(self, E, fn, reads=(), writes=()):
        self._pre(E, reads, writes)
        ins = fn()
        ev = E.mark(ins)
        self._post(E, ev, reads, writes)
        return ev

    def dma(self, Q, out, in_, sb, reads=(), writes=(), **kw):
        self._pre(Q, reads, writes)
        d = sb.dsem
        ins = Q.eng.dma_start(out=out, in_=in_, **kw)
        d.cnt += 16
        ins.then_inc(d.h, 16)
        ev = (d.h, d.cnt, d.key)
        self._post(Q, ev, reads, writes)
        return ev

    def barrier(self):
        engs = [self.pe, self.act, self.dve, self.pool, self.q_sync]
        evs = [e.last for e in (self.pe, self.act, self.dve, self.pool) if e.last]
        devs = [(d.h, d.cnt, d.key) for d in self.dsems if d.cnt]
        for E in engs:
            for ev in evs + devs:
                E.wait(ev)

    def close(self):
        self.es.close()


NCH = 16
NF = 44
EPS = 1e-6


def emit_mods(K, es, mods_ap, norm_ap, j0, tag):
    mt = K.sbuf("mt" + tag, [128, 9, 16], F32, es)
    nt = K.sbuf("nt" + tag, [128, 16], F32, es)
    gs = K.sbuf("gs" + tag, [128, 16], F32, es)
    hm = K.sbuf("hm" + tag, [128, 16], F32, es)
    bm = K.dbuf("bm" + tag)
    bn = K.dbuf("bn" + tag)
    bg = Buf("bg" + tag)
    bh = Buf("bh" + tag)
    K.dma(K.q_sync, mt[:], mods_ap, bm, writes=[bm])
    K.dma(K.q_sync, nt[:], norm_ap, bn, writes=[bn])
    K.op(K.dve, lambda: K.nc.vector.scalar_tensor_tensor(
        out=gs[:], in0=mt[:, j0 + 1, :], scalar=1.0, in1=nt[:], op0=ALU.add, op1=ALU.mult),
        reads=[bm, bn], writes=[bg])
    K.op(K.dve, lambda: K.nc.vector.tensor_scalar(
        out=hm[:], in0=mt[:, j0 + 2, :], scalar1=0.5, scalar2=None, op0=ALU.mult),
        reads=[bm], writes=[bh])
    return dict(gs=gs, sh=mt, shj=j0, hm=hm, bufs=[bm, bg, bh])


def emit_ffn(K, XT, tiles, blocks, wgu, wd, mods, tag, XTo=None, xname="X", xoname=None):
    nc = K.nc
    es = ExitStack()
    TB = max(sum(tiles[t][1] for t in blk) for blk in blocks)
    h = K.sbuf("h" + tag, [128, NCH, TB], BF16, es)
    a = K.sbuf("a" + tag, [128, NF, TB], BF16, es)
    NWG, NWD, NX = 3, 2, 4
    wg_t = [K.sbuf(f"wg{tag}{i}", [128, 2, NCH, 128], BF16, es) for i in range(NWG)]
    wd_t = [K.sbuf(f"wd{tag}{i}", [128, NF, 128], BF16, es) for i in range(NWD)]
    xr_t = [K.sbuf(f"xr{tag}{i}", [128, 512], F32, es) for i in range(NX)]
    xo_t = [K.sbuf(f"xo{tag}{i}", [128, 512], F32, es) for i in range(2)]
    sq_t = [K.sbuf(f"sq{tag}{i}", [128, 512], F32, es) for i in range(2)]
    tm_t = [K.sbuf(f"tm{tag}{i}", [128, 512], F32, es) for i in range(2)]
    sg_t = [K.sbuf(f"sg{tag}{i}", [128, 512], F32, es) for i in range(2)]
    rstd = K.sbuf("rstd" + tag, [128, 512], F32, es)
    b_wg = [K.dbuf(f"wg{tag}{i}") for i in range(NWG)]
    b_wd = [K.dbuf(f"wd{tag}{i}") for i in range(NWD)]
    b_xr = [K.dbuf(f"xr{tag}{i}") for i in range(NX)]
    b_xo = [K.dbuf(f"xo{tag}{i}") for i in range(2)]
    b_sq = [Buf() for _ in range(2)]
    b_tm = [Buf() for _ in range(2)]
    b_sg = [Buf() for _ in range(2)]
    b_rstd = Buf()
    b_h = {}
    b_a = {}
    XTv = XT.rearrange("(c p) t -> p c t", p=128)
    XTov = XTv if XTo is None else XTo.rearrange("(c p) t -> p c t", p=128)
    xoname = xoname or xname

    class _BX:
        def __init__(s, nm): s.nm = nm
        def setdefault(s, k, v): return K.xbufs.setdefault((s.nm,) + k, v)
        def __getitem__(s, k): return K.xbufs.setdefault((s.nm,) + k, Buf())
    b_X = _BX(xname)
    b_Xo = _BX(xoname)
    ps = K.ps
    bank = K.bank
    ones = K.ones_f32
    cnt = dict(xr=0, sq=0, tm=0, sg=0, xo=0, wg=0, wd=0, pb=0)

    def nxt(k, n):
        v = cnt[k]
        cnt[k] += 1
        return v % n

    for blk in blocks:
        loc = {}
        o = 0
        for t in blk:
            loc[t] = o
            o += tiles[t][1]
        wq = []

        def load_wg(f):
            s = nxt("wg", NWG)
            K.dma(K.q_pool, wg_t[s][:], wgu[f], b_wg[s], writes=[b_wg[s]])
            return s
        pre = [load_wg(f) for f in range(min(NWG - 1, NF))]
        for t in blk:
            c0, n, mi = tiles[t]
            md = mods[mi]
            bh = b_h.setdefault(t, Buf())
            pb = nxt("pb", 8)
            for c in range(NCH):
                xs = nxt("xr", NX)
                bx = b_X.setdefault((c, t), Buf())
                K.dma(K.q_sync, xr_t[xs][:, :n], XTv[:, c, c0:c0 + n], b_xr[xs],
                      reads=[bx], writes=[b_xr[xs]])
                ss = nxt("sq", 2)
                K.op(K.act, lambda: nc.scalar.activation(out=sq_t[ss][:, :n], in_=xr_t[xs][:, :n], func=AF.Square),
                     reads=[b_xr[xs]], writes=[b_sq[ss]])
                K.op(K.pe, lambda: nc.tensor.matmul(ps[:, pb * 512:pb * 512 + n], ones[:], sq_t[ss][:, :n],
                                                    start=(c == 0), stop=(c == NCH - 1)),
                     reads=[b_sq[ss]], writes=[bank[pb]])
            K.op(K.dve, lambda: nc.vector.tensor_scalar(out=rstd[:, :n], in0=ps[:, pb * 512:pb * 512 + n],
                                                        scalar1=1.0 / 2048, scalar2=EPS, op0=ALU.mult, op1=ALU.add),
                 reads=[bank[pb]], writes=[b_rstd])
            K.op(K.act, lambda: nc.scalar.activation(out=rstd[:, :n], in_=rstd[:, :n], func=AF.Sqrt),
                 reads=[b_rstd], writes=[b_rstd])
            K.op(K.dve, lambda: nc.vector.reciprocal(out=rstd[:, :n], in_=rstd[:, :n]),
                 reads=[b_rstd], writes=[b_rstd])
            for c in range(NCH):
                xs = nxt("xr", NX)
                K.dma(K.q_sync, xr_t[xs][:, :n], XTv[:, c, c0:c0 + n], b_xr[xs],
                      reads=[b_X[(c, t)]], writes=[b_xr[xs]])
                ts = nxt("tm", 2)
                K.op(K.dve, lambda: nc.vector.scalar_tensor_tensor(
                    out=tm_t[ts][:, :n], in0=xr_t[xs][:, :n], scalar=md["gs"][:, c:c + 1], in1=rstd[:, :n],
                    op0=ALU.mult, op1=ALU.mult),
                    reads=[b_xr[xs], b_rstd] + md["bufs"], writes=[b_tm[ts]])
                K.op(K.act, lambda: nc.scalar.activation(
                    out=h[:, c, loc[t]:loc[t] + n], in_=tm_t[ts][:, :n], func=AF.Identity,
                    bias=md["sh"][:, md["shj"], c:c + 1], scale=1.0),
                    reads=[b_tm[ts]] + md["bufs"], writes=[bh])
        for f in range(NF):
            s = pre.pop(0)
            if f + NWG - 1 < NF:
                pre.append(load_wg(f + NWG - 1))
            for t in blk:
                c0, n, mi = tiles[t]
                pg = nxt("pb", 8)
                pu = nxt("pb", 8)
                lo = loc[t]

                def mm(which, pbk):
                    ins = None
                    for kc in range(NCH):
                        ins = nc.tensor.matmul(ps[:, pbk * 512:pbk * 512 + n], wg_t[s][:, which, kc, :],
                                               h[:, kc, lo:lo + n], start=(kc == 0), stop=(kc == NCH - 1))
                    return ins
                K.op(K.pe, lambda: mm(0, pg), reads=[b_wg[s], b_h[t]], writes=[bank[pg]])
                K.op(K.pe, lambda: mm(1, pu), reads=[b_wg[s], b_h[t]], writes=[bank[pu]])
                gsl = nxt("sg", 2)
                K.op(K.act, lambda: nc.scalar.activation(out=sg_t[gsl][:, :n], in_=ps[:, pg * 512:pg * 512 + n], func=AF.Silu),
                     reads=[bank[pg]], writes=[b_sg[gsl]])
                ba = b_a.setdefault((f, t), Buf())
                K.op(K.dve, lambda: nc.vector.tensor_tensor(out=a[:, f, lo:lo + n], in0=sg_t[gsl][:, :n],
                                                            in1=ps[:, pu * 512:pu * 512 + n], op=ALU.mult),
                     reads=[b_sg[gsl], bank[pu]], writes=[ba])
        def load_wd(d):
            s = nxt("wd", NWD)
            K.dma(K.q_pool, wd_t[s][:], wd[d], b_wd[s], writes=[b_wd[s]])
            return s
        pred = [load_wd(0)]
        for d in range(NCH):
            s = pred.pop(0)
            if d + 1 < NCH:
                pred.append(load_wd(d + 1))
            for t in blk:
                c0, n, mi = tiles[t]
                md = mods[mi]
                lo = loc[t]
                py = nxt("pb", 8)
                xs = nxt("xr", NX)
                K.dma(K.q_sync, xr_t[xs][:, :n], XTv[:, d, c0:c0 + n], b_xr[xs],
                      reads=[b_X[(d, t)]], writes=[b_xr[xs]])

                def mmd():
                    ins = None
                    for fc in range(NF):
                        ins = nc.tensor.matmul(ps[:, py * 512:py * 512 + n], wd_t[s][:, fc, :], a[:, fc, lo:lo + n],
                                               start=(fc == 0), stop=(fc == NF - 1))
                    return ins
                K.op(K.pe, mmd, reads=[b_wd[s]] + [b_a[(fc, t)] for fc in range(NF)], writes=[bank[py]])
                xo = nxt("xo", 2)
                K.op(K.dve, lambda: nc.vector.scalar_tensor_tensor(
                    out=xo_t[xo][:, :n], in0=ps[:, py * 512:py * 512 + n], scalar=md["hm"][:, d:d + 1],
                    in1=xr_t[xs][:, :n], op0=ALU.mult, op1=ALU.add),
                    reads=[bank[py], b_xr[xs]] + md["bufs"], writes=[b_xo[xo]])
                K.dma(K.q_act, XTov[:, d, c0:c0 + n], xo_t[xo][:, :n], b_xo[xo],
                      reads=[b_xo[xo]], writes=[b_Xo[(d, t)]])
    K.barrier()
    es.close()


def setup_common(K):
    nc = K.nc
    K.ps = K.psum("ps", [128, 4096], F32)
    K.bank = [Buf(f"bank{i}") for i in range(8)]
    K.xbufs = {}
    K.pbc = 0
    K.ones_f32 = K.sbuf("ones32", [128, 128], F32)
    K.ones_bf = K.sbuf("onesbf", [128, 128], BF16)
    b = Buf("ones")
    K.op(K.pool, lambda: nc.gpsimd.memset(K.ones_f32[:], 1.0), writes=[b])
    K.op(K.pool, lambda: nc.gpsimd.memset(K.ones_bf[:], 1.0), writes=[b])
    K.barrier()


def nbank(K):
    v = K.pbc % 8
    K.pbc += 1
    return v


def bk(K, i, n=512):
    return K.ps[:, i * 512:i * 512 + n]


class XB:
    def __init__(s, K, nm):
        s.K = K
        s.nm = nm

    def __getitem__(s, k):
        return s.K.xbufs.setdefault((s.nm,) + tuple(k), Buf())


def emit_rstd(K, sq_src, C, n, D, rstd, b_rstd, sq_t, b_sq, cntr):
    nc = K.nc
    pb = nbank(K)
    for c in range(C):
        ap, rb = sq_src(c)
        ss = cntr[0] % 2
        cntr[0] += 1
        K.op(K.act, lambda: nc.scalar.activation(out=sq_t[ss][:, :n], in_=ap, func=AF.Square),
             reads=rb, writes=[b_sq[ss]])
        K.op(K.pe, lambda: nc.tensor.matmul(bk(K, pb, n), K.ones_f32[:], sq_t[ss][:, :n],
                                            start=(c == 0), stop=(c == C - 1)),
             reads=[b_sq[ss]], writes=[K.bank[pb]])
    K.op(K.dve, lambda: nc.vector.tensor_scalar(out=rstd[:, :n], in0=bk(K, pb, n),
                                                scalar1=1.0 / D, scalar2=EPS, op0=ALU.mult, op1=ALU.add),
         reads=[K.bank[pb]], writes=[b_rstd])
    K.op(K.act, lambda: nc.scalar.activation(out=rstd[:, :n], in_=rstd[:, :n], func=AF.Sqrt),
         reads=[b_rstd], writes=[b_rstd])
    K.op(K.dve, lambda: nc.vector.reciprocal(out=rstd[:, :n], in_=rstd[:, :n]),
         reads=[b_rstd], writes=[b_rstd])


def emit_norm_dram(K, XT, xname, tiles, tsel, gs_of, fin, tag):
    nc = K.nc
    es = ExitStack()
    NX = 4
    xr_t = [K.sbuf(f"nxr{tag}{i}", [128, 512], F32, es) for i in range(NX)]
    sq_t = [K.sbuf(f"nsq{tag}{i}", [128, 512], F32, es) for i in range(2)]
    tm_t = [K.sbuf(f"ntm{tag}{i}", [128, 512], F32, es) for i in range(2)]
    rstd = K.sbuf("nrstd" + tag, [128, 512], F32, es)
    b_xr = [K.dbuf(f"nxr{i}") for i in range(NX)]
    b_sq = [Buf() for _ in range(2)]
    b_tm = [Buf() for _ in range(2)]
    b_rstd = Buf()
    XTv = XT.rearrange("(c p) t -> p c t", p=128)
    bX = XB(K, xname)
    cx = [0]
    cs = [0]
    ct = [0]
    for t in tsel:
        c0, n = tiles[t][0], tiles[t][1]
        gs, gb = gs_of(t)

        def src(c):
            xs = cx[0] % NX
            cx[0] += 1
            K.dma(K.q_sync, xr_t[xs][:, :n], XTv[:, c, c0:c0 + n], b_xr[xs], reads=[bX[(c, t)]], writes=[b_xr[xs]])
            return xr_t[xs][:, :n], [b_xr[xs]]
        emit_rstd(K, src, NCH, n, 2048, rstd, b_rstd, sq_t, b_sq, cs)
        for c in range(NCH):
            ap, rb = src(c)
            ts = ct[0] % 2
            ct[0] += 1
            K.op(K.dve, lambda: nc.vector.scalar_tensor_tensor(
                out=tm_t[ts][:, :n], in0=ap, scalar=gs[:, c:c + 1], in1=rstd[:, :n], op0=ALU.mult, op1=ALU.mult),
                reads=rb + [b_rstd] + gb, writes=[b_tm[ts]])
            fin(t, c, tm_t[ts][:, :n], b_tm[ts], n)
    K.barrier()
    es.close()


def emit_norm_sbuf(K, src32, b_src, C, D, tiles, tsel, gain, b_gain, fin, tag):
    nc = K.nc
    es = ExitStack()
    sq_t = [K.sbuf(f"ssq{tag}{i}", [128, 512], F32, es) for i in range(2)]
    tm_t = [K.sbuf(f"stm{tag}{i}", [128, 512], F32, es) for i in range(2)]
    rstd = K.sbuf("srstd" + tag, [128, 512], F32, es)
    b_sq = [Buf() for _ in range(2)]
    b_tm = [Buf() for _ in range(2)]
    b_rstd = Buf()
    cs = [0]
    ct = [0]
    for t in tsel:
        c0, n = tiles[t][0], tiles[t][1]
        emit_rstd(K, lambda c: (src32[:, c, c0:c0 + n], [b_src]), C, n, D, rstd, b_rstd, sq_t, b_sq, cs)
        for c in range(C):
            ts = ct[0] % 2
            ct[0] += 1
            K.op(K.dve, lambda: nc.vector.scalar_tensor_tensor(
                out=tm_t[ts][:, :n], in0=src32[:, c, c0:c0 + n], scalar=gain[:, c:c + 1], in1=rstd[:, :n],
                op0=ALU.mult, op1=ALU.mult),
                reads=[b_src, b_rstd, b_gain], writes=[b_tm[ts]])
            fin(t, c, tm_t[ts][:, :n], b_tm[ts], n)
    K.barrier()
    es.close()


def emit_linear(K, src, b_src_of, KC, tiles, tsel_of, w, OC, evac, tag, nw=3):
    nc = K.nc
    es = ExitStack()
    w_t = [K.sbuf(f"lw{tag}{i}", [128, KC, 128], BF16, es) for i in range(nw)]
    b_w = [K.dbuf(f"lw{KC}_{i}") for i in range(nw)]
    cw = [0]

    def load(oc):
        s = cw[0] % nw
        cw[0] += 1
        K.dma(K.q_pool, w_t[s][:], w[oc], b_w[s], writes=[b_w[s]])
        return s
    pre = [load(oc) for oc in range(min(nw - 1, OC))]
    for oc in range(OC):
        s = pre.pop(0)
        if oc + nw - 1 < OC:
            pre.append(load(oc + nw - 1))
        for t in tsel_of(oc):
            c0, n = tiles[t][0], tiles[t][1]
            pb = nbank(K)

            def mm():
                ins = None
                for kc in range(KC):
                    ins = nc.tensor.matmul(bk(K, pb, n), w_t[s][:, kc, :], src[:, kc, c0:c0 + n],
                                           start=(kc == 0), stop=(kc == KC - 1))
                return ins
            K.op(K.pe, mm, reads=[b_w[s]] + b_src_of(t), writes=[K.bank[pb]])
            evac(oc, t, pb, n)
    K.barrier()
    es.close()


class OutStage:
    def __init__(self, K, es, name, shape, dt, n=3):
        self.K = K
        self.t = [K.sbuf(f"{name}{i}", shape, dt, es) for i in range(n)]
        self.b = [K.dbuf(f"{name}{i}") for i in range(n)]
        self.c = 0
        self.n = n

    def next(self):
        i = self.c % self.n
        self.c += 1
        return self.t[i], self.b[i]


ADA_COLS = 18432 // 8


def build_ada():
    nc = bass.Bass("TRN2", target_bir_lowering=False)
    cT = nc.dram_tensor("cT", [128, 16, 3], F32, kind="ExternalInput").ap()
    wada = nc.dram_tensor("wada", [2, 16, 128, ADA_COLS], F32, kind="ExternalInput").ap()
    bada = nc.dram_tensor("bada", [2, 3, ADA_COLS], F32, kind="ExternalInput").ap()
    mo = nc.dram_tensor("mods_out", [2, 3, ADA_COLS], F32, kind="ExternalOutput").ap()
    k = K(nc)
    setup_common(k)
    es = ExitStack()
    ct = k.sbuf("ct", [128, 16, 3], F32, es)
    sc = k.sbuf("sc", [128, 16, 3], F32, es)
    bt = k.sbuf("bt", [3, ADA_COLS], F32, es)
    ot = k.sbuf("ot", [3, ADA_COLS], F32, es)
    wt = [k.sbuf(f"wt{i}", [128, ADA_COLS], F32, es) for i in range(3)]
    b_ct, b_bt, b_ot = k.dbuf("ct"), k.dbuf("bt"), k.dbuf("ot")
    b_sc = Buf()
    b_wt = [k.dbuf(f"wt{i}") for i in range(3)]
    k.dma(k.q_sync, ct[:], cT, b_ct, writes=[b_ct])
    k.op(k.act, lambda: nc.scalar.activation(out=sc[:], in_=ct[:], func=AF.Silu), reads=[b_ct], writes=[b_sc])
    pieces = [(0, 512), (512, 512), (1024, 512), (1536, 512), (2048, 256)]
    wi = 0
    for l in range(2):
        k.dma(k.q_sync, bt[:], bada[l], b_bt, writes=[b_bt])
        for kc in range(16):
            s = wi % 3
            wi += 1
            k.dma(k.q_sync, wt[s][:], wada[l, kc], b_wt[s], writes=[b_wt[s]])

            def mm():
                ins = None
                for j, (o, n) in enumerate(pieces):
                    ins = nc.tensor.matmul(k.ps[0:3, j * 512:j * 512 + n], sc[:, kc, :], wt[s][:, o:o + n],
                                           start=(kc == 0), stop=(kc == 15))
                return ins
            k.op(k.pe, mm, reads=[b_wt[s], b_sc], writes=[k.bank[j] for j in range(5)])
        for j, (o, n) in enumerate(pieces):
            k.op(k.dve, lambda: nc.vector.tensor_tensor(out=ot[:, o:o + n], in0=k.ps[0:3, j * 512:j * 512 + n],
                                                        in1=bt[:, o:o + n], op=ALU.add),
                 reads=[k.bank[j], b_bt], writes=[b_ot])
        k.dma(k.q_sync, mo[l], ot[:], b_ot, reads=[b_ot])
    k.barrier()
    es.close()
    k.close()
    return nc


TILES5 = [(0, 512, 0), (512, 512, 0), (1024, 512, 0), (1536, 512, 0), (2048, 64, 1)]
XT_TILES = [0, 1, 2, 3]
N_IN_CH = 21


def build_proj():
    nc = bass.Bass("TRN2", target_bir_lowering=False)
    T = 2112
    din = lambda n, s, dt=F32: nc.dram_tensor(n, s, dt, kind="ExternalInput").ap()
    dout = lambda n, s, dt=BF16: nc.dram_tensor(n, s, dt, kind="ExternalOutput").ap()
    XT = din("xt", [2048, T])
    mods = din("mods", [2, 128, 9, 16])
    nrm = din("nrm", [128, 16])
    w_in = din("w_in", [N_IN_CH, 128, 16, 128])
    w_uq = din("w_uq", [12, 128, 6, 128])
    qn = din("qn", [128, 6])
    kvn = din("kvn", [128, 2])
    tabs = din("tabs", [4, 128, 2048])
    psw = din("psw", [2, 128, 128])
    qa_o = dout("qa_o", [8, 128, 2048])
    ka_o = dout("ka_o", [2, 128, T])
    va_o = dout("va_o", [2, 128, T])
    ckvn_o = dout("ckvn_o", [2, 128, T])
    kr_o = dout("kr_o", [128, T])
    qn_o = dout("qn_o", [8, 128, 2048])
    qr_o = dout("qr_o", [4, 128, 2048])
    k = K(nc)
    setup_common(k)
    es = ExitStack()
    md = [emit_mods(k, es, mods[i], nrm, 3, f"x{i}") for i in range(2)]
    b_h = [Buf() for _ in TILES5]
    tab = k.sbuf("tab", [128, 4, 2048], F32, es)
    pswt = k.sbuf("pswt", [128, 2, 128], F32, es)
    qnt = k.sbuf("qnt", [128, 6], F32, es)
    kvnt = k.sbuf("kvnt", [128, 2], F32, es)
    b_c = k.dbuf("consts")
    for i in range(4):
        k.dma(k.q_sync, tab[:, i, :], tabs[i], b_c, writes=[b_c])
    for i in range(2):
        k.dma(k.q_sync, pswt[:, i, :], psw[i], b_c, writes=[b_c])
    k.dma(k.q_sync, qnt[:], qn, b_c, writes=[b_c])
    k.dma(k.q_sync, kvnt[:], kvn, b_c, writes=[b_c])

    cq32 = k.sbuf("cq32", [128, 6, 2048], F32, es)
    ckv32 = k.sbuf("ckv32", [128, 2, T], F32, es)
    b_cq, b_ckv = Buf(), Buf()
    stage = OutStage(k, es, "ost", [128, 512], BF16, 3)
    qf = OutStage(k, es, "qf", [128, 512], F32, 2)
    t1r = OutStage(k, es, "t1r", [128, 512], F32, 2)
    t2r = OutStage(k, es, "t2r", [128, 512], F32, 2)
    es_h = ExitStack()
    h = k.sbuf("hmix", [128, 16, T], BF16, es_h)

    def fin_h(t, c, tm, b_tm, n):
        c0 = TILES5[t][0]
        m = md[TILES5[t][2]]
        k.op(k.act, lambda: nc.scalar.activation(out=h[:, c, c0:c0 + n], in_=tm, func=AF.Identity,
                                                 bias=m["sh"][:, 4 - 1, c:c + 1], scale=1.0),
             reads=[b_tm] + m["bufs"], writes=[b_h[t]])
    emit_norm_dram(k, XT, "X", TILES5, [0, 1, 2, 3, 4], lambda t: (md[TILES5[t][2]]["gs"], md[TILES5[t][2]]["bufs"]),
                   fin_h, "mix")

    def ev_copy(pb, n, dest):
        st, bst = stage.next()
        k.op(k.act, lambda: nc.scalar.activation(out=st[:, :n], in_=bk(k, pb, n), func=AF.Identity),
             reads=[k.bank[pb]], writes=[bst])
        k.dma(k.q_sync, dest, st[:, :n], bst, reads=[bst])

    def ev_rope(pb, n, c0, ti, dest):
        q, bq = qf.next()
        k.op(k.act, lambda: nc.scalar.activation(out=q[:, :n], in_=bk(k, pb, n), func=AF.Identity),
             reads=[k.bank[pb]], writes=[bq])
        pb2 = nbank(k)
        k.op(k.pe, lambda: nc.tensor.matmul(bk(k, pb2, n), pswt[:, ti, :], q[:, :n], start=True, stop=True),
             reads=[bq, b_c], writes=[k.bank[pb2]])
        a1, b1 = t1r.next()
        a2, b2 = t2r.next()
        k.op(k.dve, lambda: nc.vector.tensor_tensor(out=a1[:, :n], in0=q[:, :n], in1=tab[:, 2 * ti, c0:c0 + n], op=ALU.mult),
             reads=[bq, b_c], writes=[b1])
        k.op(k.dve, lambda: nc.vector.tensor_tensor(out=a2[:, :n], in0=bk(k, pb2, n), in1=tab[:, 2 * ti + 1, c0:c0 + n], op=ALU.mult),
             reads=[k.bank[pb2], b_c], writes=[b2])
        st, bst = stage.next()
        k.op(k.dve, lambda: nc.vector.tensor_tensor(out=st[:, :n], in0=a1[:, :n], in1=a2[:, :n], op=ALU.add),
             reads=[b1, b2], writes=[bst])
        k.dma(k.q_sync, dest, st[:, :n], bst, reads=[bst])

    def evac1(oc, t, pb, n):
        c0 = TILES5[t][0]
        ctx = (t == 4)
        if oc < 8:
            ev_rope(pb, n, c0, 0, qa_o[oc][:, c0:c0 + n])
        elif oc < 10:
            if ctx:
                ev_copy(pb, n, ka_o[oc - 8][:, c0:c0 + n])
            else:
                ev_rope(pb, n, c0, 0, ka_o[oc - 8][:, c0:c0 + n])
        elif oc < 12:
            ev_copy(pb, n, va_o[oc - 10][:, c0:c0 + n])
        elif oc < 18:
            k.op(k.act, lambda: nc.scalar.activation(out=cq32[:, oc - 12, c0:c0 + n], in_=bk(k, pb, n), func=AF.Identity),
                 reads=[k.bank[pb]], writes=[b_cq])
        elif oc < 20:
            k.op(k.act, lambda: nc.scalar.activation(out=ckv32[:, oc - 18, c0:c0 + n], in_=bk(k, pb, n), func=AF.Identity),
                 reads=[k.bank[pb]], writes=[b_ckv])
        else:
            if ctx:
                ev_copy(pb, n, kr_o[:, c0:c0 + n])
            else:
                ev_rope(pb, n, c0, 1, kr_o[:, c0:c0 + n])
    emit_linear(k, h, lambda t: [b_h[t]], 16, TILES5,
                lambda oc: XT_TILES if (oc < 8 or 12 <= oc < 18) else [0, 1, 2, 3, 4], w_in, N_IN_CH, evac1, "in")
    es_h.close()
    cqn = k.sbuf("cqn", [128, 6, 2048], BF16, es)
    b_cqn = Buf()

    def fin_cq(t, c, tm, b_tm, n):
        c0 = TILES5[t][0]
        k.op(k.act, lambda: nc.scalar.activation(out=cqn[:, c, c0:c0 + n], in_=tm, func=AF.Identity),
             reads=[b_tm], writes=[b_cqn])
    emit_norm_sbuf(k, cq32, b_cq, 6, 768, TILES5, XT_TILES, qnt, b_c, fin_cq, "cq")

    def fin_ckv(t, c, tm, b_tm, n):
        c0 = TILES5[t][0]
        st, bst = stage.next()
        k.op(k.act, lambda: nc.scalar.activation(out=st[:, :n], in_=tm, func=AF.Identity),
             reads=[b_tm], writes=[bst])
        k.dma(k.q_sync, ckvn_o[c][:, c0:c0 + n], st[:, :n], bst, reads=[bst])
    emit_norm_sbuf(k, ckv32, b_ckv, 2, 256, TILES5, [0, 1, 2, 3, 4], kvnt, b_c, fin_ckv, "ckv")

    def evac2(oc, t, pb, n):
        c0 = TILES5[t][0]
        if oc < 8:
            ev_copy(pb, n, qn_o[oc][:, c0:c0 + n])
        else:
            ev_rope(pb, n, c0, 1, qr_o[oc - 8][:, c0:c0 + n])
    emit_linear(k, cqn, lambda t: [b_cqn], 6, TILES5, lambda oc: XT_TILES, w_uq, 12, evac2, "uq")
    k.barrier()
    es.close()
    k.close()
    return nc


def lay_mods(m):
    return np.ascontiguousarray(m.reshape(9, 16, 128).transpose(2, 0, 1))


def lay_vec(g, C=16):
    return np.ascontiguousarray(g.reshape(C, 128).T)


def lay_wgu(Wg, Wu):
    a = Wg.reshape(16, 128, 44, 128).transpose(2, 1, 0, 3)
    b = Wu.reshape(16, 128, 44, 128).transpose(2, 1, 0, 3)
    return np.ascontiguousarray(np.stack([a, b], axis=2))


def lay_wd(Wd):
    return np.ascontiguousarray(Wd.reshape(44, 128, 16, 128).transpose(2, 1, 0, 3))


def lay_lin(W):
    KC, OC = W.shape[0] // 128, W.shape[1] // 128
    return np.ascontiguousarray(W.reshape(KC, 128, OC, 128).transpose(2, 1, 0, 3))


def sel_w_in(w):
    cols = np.concatenate([np.arange(0, 2560), np.arange(2560, 2624), np.arange(2560, 2624)])
    return lay_lin(w[:, cols])


def sel_w_uq(w):
    cols = []
    for hh in range(8):
        cols.append(np.arange(hh * 192, hh * 192 + 128))
    for hh in range(8):
        cols.append(np.arange(hh * 192 + 128, hh * 192 + 192))
    return lay_lin(w[:, np.concatenate(cols)])


def rope_tables(s0):
    t = np.arange(s0, s0 + 2048)
    row = (t // 64).astype(np.float32)
    col = (t % 64).astype(np.float32)
    out = []
    for nf, rep in ((32, 1), (16, 2)):
        inv = (np.float32(10000.0) ** (-np.arange(nf, dtype=np.float32) / np.float32(nf))).astype(np.float32)
        ang = np.concatenate([row[:, None] * inv, col[:, None] * inv], axis=-1).astype(np.float32)
        c = np.cos(ang).astype(np.float32).T
        s = np.sin(ang).astype(np.float32).T
        out.append(np.tile(np.concatenate([c, c], 0), (rep, 1)))
        out.append(np.tile(np.concatenate([-s, s], 0), (rep, 1)))
    return np.ascontiguousarray(np.stack(out)).astype(np.float32)


def swap_mats():
    A = np.zeros((128, 128), np.float32)
    B = np.zeros((128, 128), np.float32)
    for m in range(128):
        A[(m + 64) % 128, m] = 1.0
        B[(m // 64) * 64 + ((m % 64) + 32) % 64, m] = 1.0
    return np.stack([A, B])


def run(nc, ins):
    return run_bass_kernel_spmd(nc, ins, core_ids=list(range(8))).results


def host_ada(c, c_ctx, w_ada, b_ada):
    cT = np.ascontiguousarray(np.stack([c[0], c[1], c_ctx], 0).reshape(3, 16, 128).transpose(2, 1, 0))
    ins = []
    for r in range(8):
        sl = slice(r * ADA_COLS, (r + 1) * ADA_COLS)
        ins.append(dict(cT=cT, wada=np.ascontiguousarray(w_ada[:, :, sl].reshape(2, 16, 128, ADA_COLS)),
                        bada=np.ascontiguousarray(np.broadcast_to(b_ada[:, None, sl], (2, 3, ADA_COLS)))))
    res = run(build_ada(), ins)
    full = np.concatenate([res[r]["mods_out"] for r in range(8)], axis=2)
    return full


NKT = 66
NKEY = 8448


def build_attn():
    nc = bass.Bass("TRN2", target_bir_lowering=False)
    din = lambda n, s, dt=BF16: nc.dram_tensor(n, s, dt, kind="ExternalInput").ap()
    qa = din("qa", [128, 2, 16, 512])
    kae = din("kae", [128, 2, 2560])
    vae = din("vae", [128, 2, 20, 128])
    masks = din("masks", [128, 4, 512])
    sink = din("sink", [128, 8], F32)
    ckvn = din("ckvn", [128, 2, NKEY])
    kr = din("kr", [128, NKEY])
    qn = din("qn", [8, 128, 2048])
    qr = din("qr", [4, 128, 2048])
    wk = din("wk", [8, 128, 2, 128], F32)
    wv = din("wv", [8, 128, 2, 128], F32)
    ocat = nc.dram_tensor("ocat", [16, 128, 2048], BF16, kind="ExternalOutput").ap()
    k = K(nc)
    setup_common(k)
    es = ExitStack()
    oc_sb = k.sbuf("oc_sb", [128, 16, 2048], BF16, es)
    b_oc = [Buf() for _ in range(16)]
    b_ocd = k.dbuf("ocd")
    den = OutStage(k, es, "den", [128, 512], F32, 2)
    pring = OutStage(k, es, "pring", [128, 512], BF16, 3)
    SA = 128.0 ** -0.5
    SB = 192.0 ** -0.5
    esa = ExitStack()
    qa_sb = k.sbuf("qa_sb", [128, 2, 16, 512], BF16, esa)
    kae_sb = k.sbuf("kae_sb", [128, 2, 2560], BF16, esa)
    vae_sb = k.sbuf("vae_sb", [128, 2, 20, 128], BF16, esa)
    mk_sb = k.sbuf("mk_sb", [128, 4, 512], BF16, esa)
    sk_sb = k.sbuf("sk_sb", [128, 8], F32, esa)
    pa = [k.sbuf(f"pa{i}", [128, 5, 512], BF16, esa) for i in range(2)]
    b_pa = [[Buf() for _ in range(5)] for _ in range(2)]
    b_in = k.dbuf("attn_in")
    b_sk = Buf()
    for (t_, s_) in ((qa_sb, qa), (kae_sb, kae), (vae_sb, vae), (mk_sb, masks), (sk_sb, sink)):
        k.dma(k.q_sync, t_[:], s_, b_in, writes=[b_in])
    k.op(k.act, lambda: nc.scalar.activation(out=sk_sb[:], in_=sk_sb[:], func=AF.Exp), reads=[b_in], writes=[b_sk])
    it = 0
    for g in range(2):
        for qb in range(16):
            pi = it % 2
            it += 1
            blks = [qb, qb + 1, qb + 2, 18, 19]
            sb = []
            for j, bl in enumerate(blks):
                pb = nbank(k)
                sb.append(pb)
                k.op(k.pe, lambda: nc.tensor.matmul(bk(k, pb), kae_sb[:, g, bl * 128:(bl + 1) * 128], qa_sb[:, g, qb, :],
                                                    start=True, stop=True),
                     reads=[b_in], writes=[k.bank[pb]])
            for j in range(5):
                k.op(k.act, lambda: nc.scalar.activation(out=pa[pi][:, j, :], in_=bk(k, sb[j]), func=AF.Exp, scale=SA),
                     reads=[k.bank[sb[j]]], writes=[b_pa[pi][j]])
            m0 = 2 if qb == 0 else 0
            m2 = 3 if qb == 15 else 1
            k.op(k.dve, lambda: nc.vector.tensor_tensor(out=pa[pi][:, 0, :], in0=pa[pi][:, 0, :], in1=mk_sb[:, m0, :], op=ALU.mult),
                 reads=[b_in], writes=[b_pa[pi][0]])
            k.op(k.dve, lambda: nc.vector.tensor_tensor(out=pa[pi][:, 2, :], in0=pa[pi][:, 2, :], in1=mk_sb[:, m2, :], op=ALU.mult),
                 reads=[b_in], writes=[b_pa[pi][2]])
            po = nbank(k)
            pl = nbank(k)

            def mo():
                ins = None
                for j, bl in enumerate(blks):
                    ins = nc.tensor.matmul(bk(k, po), vae_sb[:, g, bl, :], pa[pi][:, j, :], start=(j == 0), stop=(j == 4))
                return ins

            def ml():
                ins = None
                for j in range(5):
                    ins = nc.tensor.matmul(bk(k, pl), k.ones_bf[:], pa[pi][:, j, :], start=(j == 0), stop=(j == 4))
                return ins
            k.op(k.pe, mo, reads=b_pa[pi] + [b_in], writes=[k.bank[po]])
            k.op(k.pe, ml, reads=b_pa[pi], writes=[k.bank[pl]])
            dn, bdn = den.next()
            for r in range(4):
                hh = 4 * g + r
                k.op(k.dve, lambda: nc.vector.tensor_scalar(out=dn[:, r * 128:(r + 1) * 128], in0=k.ps[:, pl * 512 + r * 128:pl * 512 + (r + 1) * 128],
                                                            scalar1=sk_sb[:, hh:hh + 1], scalar2=None, op0=ALU.add),
                     reads=[k.bank[pl], b_sk], writes=[bdn])
            k.op(k.dve, lambda: nc.vector.reciprocal(out=dn[:], in_=dn[:]), reads=[bdn], writes=[bdn])
            for r in range(4):
                hh = 4 * g + r
                k.op(k.dve, lambda: nc.vector.tensor_tensor(out=oc_sb[:, hh, qb * 128:(qb + 1) * 128],
                                                            in0=k.ps[:, po * 512 + r * 128:po * 512 + (r + 1) * 128],
                                                            in1=dn[:, r * 128:(r + 1) * 128], op=ALU.mult),
                     reads=[k.bank[po], bdn], writes=[b_oc[hh]])
    k.barrier()
    esa.close()
    ckvn_sb = k.sbuf("ckvn_sb", [128, 2, NKEY], BF16, es)
    kr_sb = k.sbuf("kr_sb", [128, NKEY], BF16, es)
    kh = k.sbuf("kh", [128, NKEY], BF16, es)
    vh = k.sbuf("vh", [128, NKT, 128], BF16, es)
    qn_sb = k.sbuf("qn_sb", [128, 2048], BF16, es)
    qr_sb = k.sbuf("qr_sb", [128, 2048], BF16, es)
    wk_sb = k.sbuf("wk_sb", [128, 2, 128], BF16, es)
    wv_sb = k.sbuf("wv_sb", [128, 2, 128], BF16, es)
    b_lat = k.dbuf("lat")
    b_q = k.dbuf("bq")
    b_w = k.dbuf("bw")
    b_kh, b_vh = Buf(), Buf()
    k.dma(k.q_sync, ckvn_sb[:], ckvn, b_lat, writes=[b_lat])
    k.dma(k.q_sync, kr_sb[:], kr, b_lat, writes=[b_lat])
    for hh in range(8):
        k.dma(k.q_pool, wk_sb[:], wk[hh], b_w, writes=[b_w])
        k.dma(k.q_pool, wv_sb[:], wv[hh], b_w, writes=[b_w])
        k.dma(k.q_sync, qn_sb[:], qn[hh], b_q, writes=[b_q])
        k.dma(k.q_sync, qr_sb[:], qr[hh // 2], b_q, writes=[b_q])
        c0 = 0
        while c0 < NKEY:
            n = min(512, NKEY - c0)
            pb = nbank(k)

            def mmk():
                ins = None
                for kc in range(2):
                    ins = nc.tensor.matmul(bk(k, pb, n), wk_sb[:, kc, :], ckvn_sb[:, kc, c0:c0 + n], start=(kc == 0), stop=(kc == 1))
                return ins
            k.op(k.pe, mmk, reads=[b_w, b_lat], writes=[k.bank[pb]])
            k.op(k.act, lambda: nc.scalar.activation(out=kh[:, c0:c0 + n], in_=bk(k, pb, n), func=AF.Identity),
                 reads=[k.bank[pb]], writes=[b_kh])
            c0 += n
        kt = 0
        while kt < NKT:
            m = min(4, NKT - kt)
            pb = nbank(k)

            def mmv():
                ins = None
                for j in range(m):
                    for kc in range(2):
                        ins = nc.tensor.matmul(k.ps[:, pb * 512 + j * 128:pb * 512 + (j + 1) * 128],
                                               ckvn_sb[:, kc, (kt + j) * 128:(kt + j + 1) * 128], wv_sb[:, kc, :],
                                               start=(kc == 0), stop=(kc == 1))
                return ins
            k.op(k.pe, mmv, reads=[b_w, b_lat], writes=[k.bank[pb]])
            k.op(k.dve, lambda: nc.vector.tensor_copy(out=vh[:, kt:kt + m, :], in_=k.ps[:, pb * 512:pb * 512 + m * 128].rearrange("p (a b) -> p a b", b=128)),
                 reads=[k.bank[pb]], writes=[b_vh])
            kt += m
        r0 = 64 * (hh % 2)
        for qt in range(4):
            qs = slice(qt * 512, (qt + 1) * 512)
            po = nbank(k)
            pl = nbank(k)

            def smm(kt, pb):
                nc.tensor.matmul(bk(k, pb), kh[:, kt * 128:(kt + 1) * 128], qn_sb[:, qs], start=True, stop=False)
                return nc.tensor.matmul(bk(k, pb), kr_sb[r0:r0 + 64, kt * 128:(kt + 1) * 128], qr_sb[r0:r0 + 64, qs],
                                        start=False, stop=True)
            sbank = {}

            def issue_s(kt):
                pb = nbank(k)
                while pb in (po, pl):
                    pb = nbank(k)
                sbank[kt] = pb
                k.op(k.pe, lambda: smm(kt, pb), reads=[b_kh, b_q, b_lat], writes=[k.bank[pb]])
            issue_s(0)
            issue_s(1)
            for kt in range(NKT):
                if kt + 2 < NKT:
                    issue_s(kt + 2)
                pb = sbank[kt]
                p_t, b_p = pring.next()
                k.op(k.act, lambda: nc.scalar.activation(out=p_t[:], in_=bk(k, pb), func=AF.Exp, scale=SB),
                     reads=[k.bank[pb]], writes=[b_p])

                def pv():
                    nc.tensor.matmul(bk(k, po), vh[:, kt, :], p_t[:], start=(kt == 0), stop=(kt == NKT - 1))
                    return nc.tensor.matmul(bk(k, pl), k.ones_bf[:], p_t[:], start=(kt == 0), stop=(kt == NKT - 1))
                k.op(k.pe, pv, reads=[b_p, b_vh], writes=[k.bank[po], k.bank[pl]])
            dn, bdn = den.next()
            k.op(k.dve, lambda: nc.vector.reciprocal(out=dn[:], in_=bk(k, pl)), reads=[k.bank[pl]], writes=[bdn])
            k.op(k.dve, lambda: nc.vector.tensor_tensor(out=oc_sb[:, 8 + hh, qs], in0=bk(k, po), in1=dn[:], op=ALU.mult),
                 reads=[k.bank[po], bdn], writes=[b_oc[8 + hh]])
    for c in range(16):
        k.dma(k.q_sync, ocat[c], oc_sb[:, c, :], b_ocd, reads=[b_oc[c]])
    k.barrier()
    es.close()
    k.close()
    return nc


TILES4 = [(0, 512, 0), (512, 512, 0), (1024, 512, 0), (1536, 512, 0)]
BLOCKS4 = [[0, 1], [2, 3]]


def emit_resid_evac(k, es, XTi, xin_name, XTo, xo_name, scal_of, tag):
    nc = k.nc
    xr = OutStage(k, es, "rxr" + tag, [128, 512], F32, 3)
    xo = OutStage(k, es, "rxo" + tag, [128, 512], F32, 2)
    XTiv = XTi.rearrange("(c p) t -> p c t", p=128)
    XTov = XTo.rearrange("(c p) t -> p c t", p=128)
    bXi, bXo = XB(k, xin_name), XB(k, xo_name)

    def evac(oc, t, pb, n, tiles=TILES4):
        c0 = tiles[t][0]
        xi, bxi = xr.next()
        k.dma(k.q_sync, xi[:, :n], XTiv[:, oc, c0:c0 + n], bxi, reads=[bXi[(oc, t)]], writes=[bxi])
        xt, bxt = xo.next()
        sc, sb = scal_of(oc)
        k.op(k.dve, lambda: nc.vector.scalar_tensor_tensor(out=xt[:, :n], in0=bk(k, pb, n), scalar=sc, in1=xi[:, :n],
                                                           op0=ALU.mult, op1=ALU.add),
             reads=[k.bank[pb], bxi] + sb, writes=[bxt])
        k.dma(k.q_act, XTov[:, oc, c0:c0 + n], xt[:, :n], bxt, reads=[bxt], writes=[bXo[(oc, t)]])
    return evac


def build_post():
    nc = bass.Bass("TRN2", target_bir_lowering=False)
    din = lambda n, s, dt=F32: nc.dram_tensor(n, s, dt, kind="ExternalInput").ap()
    ocat = din("ocat", [16, 128, 2048], BF16)
    XTi = din("xt", [2048, 2048])
    w_out = din("w_out", [16, 128, 16, 128])
    mods0 = din("mods0", [128, 9, 16])
    mods1 = din("mods1", [128, 9, 16])
    n_f2 = din("n_f2", [128, 16])
    n_f1 = din("n_f1", [128, 16])
    n_mx = din("n_mx", [128, 16])
    wgu_a = din("wgu_a", [44, 128, 2, 16, 128])
    wd_a = din("wd_a", [16, 128, 44, 128])
    wgu_b = din("wgu_b", [44, 128, 2, 16, 128])
    wd_b = din("wd_b", [16, 128, 44, 128])
    XTo = nc.dram_tensor("xt_out", [2048, 2048], F32, kind="ExternalOutput").ap()
    h1o = nc.dram_tensor("h1_out", [2048, 2048], F32, kind="ExternalOutput").ap()
    k = K(nc)
    setup_common(k)
    es = ExitStack()
    mdA = emit_mods(k, es, mods0, n_f2, 6, "A")
    mdB = emit_mods(k, es, mods1, n_f1, 0, "B")
    mdC = emit_mods(k, es, mods1, n_mx, 3, "C")
    es1 = ExitStack()
    oc_sb = k.sbuf("oc_sb", [128, 16, 2048], BF16, es1)
    b_oc = k.dbuf("ocl")
    for c in range(16):
        k.dma(k.q_sync, oc_sb[:, c, :], ocat[c], b_oc, writes=[b_oc])
    ev = emit_resid_evac(k, es1, XTi, "Xi", XTo, "Xo", lambda oc: (mdA["sh"][:, 5, oc:oc + 1], mdA["bufs"]), "wo")
    emit_linear(k, oc_sb, lambda t: [b_oc], 16, TILES4, lambda oc: [0, 1, 2, 3], w_out, 16, ev, "wo")
    es1.close()
    emit_ffn(k, XTo, TILES4, BLOCKS4, wgu_a, wd_a, [mdA], "fa", xname="Xo")
    emit_ffn(k, XTo, TILES4, BLOCKS4, wgu_b, wd_b, [mdB], "fb", xname="Xo")
    es2 = ExitStack()
    st = OutStage(k, es2, "h1st", [128, 512], F32, 3)
    h1v = h1o.rearrange("(c p) t -> p c t", p=128)

    def fin(t, c, tm, b_tm, n):
        c0 = TILES4[t][0]
        s_, bs = st.next()
        k.op(k.act, lambda: nc.scalar.activation(out=s_[:, :n], in_=tm, func=AF.Identity,
                                                 bias=mdC["sh"][:, 3, c:c + 1], scale=1.0),
             reads=[b_tm] + mdC["bufs"], writes=[bs])
        k.dma(k.q_sync, h1v[:, c, c0:c0 + n], s_[:, :n], bs, reads=[bs])
    emit_norm_dram(k, XTo, "Xo", TILES4, [0, 1, 2, 3], lambda t: (mdC["gs"], mdC["bufs"]), fin, "pn")
    es2.close()
    k.barrier()
    es.close()
    k.close()
    return nc


HALO_L, HALO_R = 8, 7
EXTW = 512 + HALO_L + HALO_R


def build_pool():
    nc = bass.Bass("TRN2", target_bir_lowering=False)
    din = lambda n, s, dt=F32: nc.dram_tensor(n, s, dt, kind="ExternalInput").ap()
    XTi = din("xt", [2048, 2048])
    h1e = din("h1e", [2048, 2048 + HALO_L + HALO_R])
    invc = din("invc", [128, 4, 2048])
    pw = din("pw", [16, 128, 4, 128])
    pscale = din("pscale", [128, 16])
    mods1 = din("mods1", [128, 9, 16])
    n_f2 = din("n_f2", [128, 16])
    n_fin = din("n_fin", [128, 16])
    wgu = din("wgu", [44, 128, 2, 16, 128])
    wd = din("wd", [16, 128, 44, 128])
    XTo = nc.dram_tensor("xt_out", [2048, 2048], F32, kind="ExternalOutput").ap()
    outT = nc.dram_tensor("outT", [2048, 2048], F32, kind="ExternalOutput").ap()
    k = K(nc)
    setup_common(k)
    es = ExitStack()
    mdA = emit_mods(k, es, mods1, n_f2, 6, "A")
    es1 = ExitStack()
    pw_sb = k.sbuf("pw_sb", [128, 16, 4, 128], BF16, es1)
    ic_sb = k.sbuf("ic_sb", [128, 4, 2048], F32, es1)
    ps_sb = k.sbuf("ps_sb", [128, 16], F32, es1)
    gsc = k.sbuf("gsc", [128, 16], F32, es1)
    hx = k.sbuf("hx", [128, 16, EXTW], F32, es1)
    dT = k.sbuf("dT", [128, 16, 512], BF16, es1)
    sw = [k.sbuf(f"sw{i}", [128, EXTW], F32, es1) for i in range(4)]
    tmpd = k.sbuf("tmpd", [128, 512], F32, es1)
    b_c = k.dbuf("pconst")
    b_pw = k.dbuf("ppw")
    b_hx = k.dbuf("phx")
    b_gsc, b_sw, b_tmp = Buf(), Buf(), Buf()
    b_d = [Buf() for _ in range(16)]
    for i in range(16):
        k.dma(k.q_pool, pw_sb[:, i, :, :], pw[i], b_pw, writes=[b_pw])
    k.dma(k.q_sync, ic_sb[:], invc, b_c, writes=[b_c])
    k.dma(k.q_sync, ps_sb[:], pscale, b_c, writes=[b_c])
    k.op(k.dve, lambda: nc.vector.tensor_tensor(out=gsc[:], in0=ps_sb[:], in1=mdA["sh"][:, 5, :], op=ALU.mult),
         reads=[b_c] + mdA["bufs"], writes=[b_gsc])
    ev = emit_resid_evac(k, es1, XTi, "Xi", XTo, "Xo", lambda oc: (gsc[:, oc:oc + 1], [b_gsc]), "pl")
    h1v = h1e.rearrange("(c p) t -> p c t", p=128)
    for t in range(4):
        c0 = TILES4[t][0]
        k.dma(k.q_sync, hx[:], h1v[:, :, c0:c0 + EXTW], b_hx, reads=[], writes=[b_hx])
        for c in range(16):
            g = c // 4
            src = hx[:, c, :]
            cur = src
            k.op(k.dve, lambda: nc.vector.tensor_tensor(out=sw[0][:, 1:EXTW], in0=src[:, 0:EXTW - 1], in1=src[:, 1:EXTW], op=ALU.add),
                 reads=[b_hx], writes=[b_sw])
            lo, hi = 1, EXTW
            cur = sw[0]
            for lvl in range(1, g + 1):
                sh_ = 1 << (lvl - 1)
                nlo, nhi = lo + sh_, hi - sh_
                k.op(k.dve, lambda: nc.vector.tensor_tensor(out=sw[lvl][:, nlo:nhi], in0=cur[:, nlo - sh_:nhi - sh_],
                                                            in1=cur[:, nlo + sh_:nhi + sh_], op=ALU.add),
                     reads=[b_sw], writes=[b_sw])
                cur = sw[lvl]
                lo, hi = nlo, nhi
            k.op(k.dve, lambda: nc.vector.tensor_tensor(out=tmpd[:], in0=cur[:, HALO_L:HALO_L + 512], in1=ic_sb[:, g, c0:c0 + 512], op=ALU.mult),
                 reads=[b_sw, b_c], writes=[b_tmp])
            k.op(k.dve, lambda: nc.vector.tensor_tensor(out=dT[:, c, :], in0=tmpd[:], in1=src[:, HALO_L:HALO_L + 512], op=ALU.subtract),
                 reads=[b_tmp, b_hx], writes=[b_d[c]])
        for g in range(4):
            for oc in range(4):
                pb = nbank(k)

                def mm():
                    ins = None
                    for kc in range(4):
                        ins = nc.tensor.matmul(bk(k, pb), pw_sb[:, g * 4 + oc, kc, :], dT[:, g * 4 + kc, :], start=(kc == 0), stop=(kc == 3))
                    return ins
                k.op(k.pe, mm, reads=[b_pw] + [b_d[g * 4 + kc] for kc in range(4)], writes=[k.bank[pb]])
                ev(g * 4 + oc, t, pb, 512)
    k.barrier()
    es1.close()
    emit_ffn(k, XTo, TILES4, BLOCKS4, wgu, wd, [mdA], "fc", xname="Xo")
    es2 = ExitStack()
    nf = k.sbuf("nf", [128, 16], F32, es2)
    b_nf = k.dbuf("nf")
    k.dma(k.q_sync, nf[:], n_fin, b_nf, writes=[b_nf])
    st = OutStage(k, es2, "fst", [128, 512], F32, 3)
    ov = outT.rearrange("(c p) t -> p c t", p=128)

    def fin(t, c, tm, b_tm, n):
        c0 = TILES4[t][0]
        k.dma(k.q_act, ov[:, c, c0:c0 + n], tm, b_tm, reads=[b_tm])
    def fin2(t, c, tm, b_tm, n):
        c0 = TILES4[t][0]
        s_, bs = st.next()
        k.op(k.act, lambda: nc.scalar.activation(out=s_[:, :n], in_=tm, func=AF.Identity), reads=[b_tm], writes=[bs])
        k.dma(k.q_sync, ov[:, c, c0:c0 + n], s_[:, :n], bs, reads=[bs])
    emit_norm_dram(k, XTo, "Xo", TILES4, [0, 1, 2, 3], lambda t: (nf, [b_nf]), fin2, "fn")
    es2.close()
    k.barrier()
    es.close()
    k.close()
    return nc


BLOCKS5 = [[0, 1], [2, 3, 4]]


def build_ffn1():
    nc = bass.Bass("TRN2", target_bir_lowering=False)
    T = 2112
    XTi = nc.dram_tensor("xt", [2048, T], F32, kind="ExternalInput").ap()
    XTo = nc.dram_tensor("xt_out", [2048, T], F32, kind="ExternalOutput").ap()
    wgu = nc.dram_tensor("wgu", [44, 128, 2, 16, 128], F32, kind="ExternalInput").ap()
    wd = nc.dram_tensor("wd", [16, 128, 44, 128], F32, kind="ExternalInput").ap()
    mods = nc.dram_tensor("mods", [2, 128, 9, 16], F32, kind="ExternalInput").ap()
    nrm = nc.dram_tensor("nrm", [128, 16], F32, kind="ExternalInput").ap()
    k = K(nc)
    setup_common(k)
    es = ExitStack()
    md = [emit_mods(k, es, mods[i], nrm, 0, f"m{i}") for i in range(2)]
    emit_ffn(k, XTi, TILES5, BLOCKS5, wgu, wd, md, "f1", XTo=XTo, xname="Xi", xoname="Xo")
    k.barrier()
    es.close()
    k.close()
    return nc


def kernel(x, c, ctx, c_ctx, w_ada, b_ada, norm_ffn1, norm_mix, norm_ffn2,
           ffn1_w_gate, ffn1_w_up, ffn1_w_down, ffn2_w_gate, ffn2_w_up, ffn2_w_down,
           attn_w_in, attn_sink, mla_q_norm, mla_w_uq, mla_kv_norm, mla_w_ukv, attn_w_out,
           pool_w, pool_scale, final_norm):
    f = lambda a: np.ascontiguousarray(np.asarray(a, dtype=np.float32))
    x, c, ctx, c_ctx, w_ada, b_ada = f(x), f(c), f(ctx), f(c_ctx), f(w_ada), f(b_ada)
    norm_ffn1, norm_mix, norm_ffn2 = f(norm_ffn1), f(norm_mix), f(norm_ffn2)
    NC = 8
    mods = host_ada(c, c_ctx, w_ada, b_ada)
    mx = [[lay_mods(mods[l, b]) for b in range(2)] for l in range(2)]
    mcx = lay_mods(mods[0, 2])
    wgu = lay_wgu(f(ffn1_w_gate)[0], f(ffn1_w_up)[0])
    wd = lay_wd(f(ffn1_w_down)[0])
    ins = []
    for r in range(NC):
        b, q = divmod(r, 4)
        xt = np.concatenate([x[b, q * 2048:(q + 1) * 2048].T, ctx[b, q * 64:(q + 1) * 64].T], axis=1)
        ins.append(dict(xt=np.ascontiguousarray(xt), wgu=wgu, wd=wd, mods=np.stack([mx[0][b], mcx]),
                        nrm=lay_vec(norm_ffn1[0])))
    r1 = run(build_ffn1(), ins)
    xt1 = [np.asarray(r1[r]["xt_out"]) for r in range(NC)]
    del ins, r1
    w_in_l = sel_w_in(f(attn_w_in)[0])
    w_uq_l = sel_w_uq(f(mla_w_uq)[0])
    psw = swap_mats()
    ins = []
    for r in range(NC):
        b, q = divmod(r, 4)
        ins.append(dict(xt=xt1[r], mods=np.stack([mx[0][b], mcx]), nrm=lay_vec(norm_mix[0]), w_in=w_in_l, w_uq=w_uq_l,
                        qn=lay_vec(f(mla_q_norm)[0], 6), kvn=lay_vec(f(mla_kv_norm)[0], 2),
                        tabs=rope_tables(q * 2048), psw=psw))
    P = run(build_proj(), ins)
    P = [{kk: np.asarray(v) for kk, v in P[r].items()} for r in range(NC)]
    del ins
    w_ukv = f(mla_w_ukv)[0]
    ck = np.concatenate([np.arange(h * 256, h * 256 + 128) for h in range(8)])
    wk_l = lay_lin(w_ukv[:, ck])
    wv_l = lay_lin(w_ukv[:, ck + 128])
    bdt = P[0]["qa_o"].dtype
    kk_ = np.arange(128)[:, None]
    qq_ = np.arange(128)[None, :]
    m_prev = np.tile((kk_ >= qq_).astype(np.float32), (1, 4))
    m_next = np.tile((kk_ <= qq_).astype(np.float32), (1, 4))
    zero = np.zeros_like(m_prev)
    sink_l = np.ascontiguousarray(np.broadcast_to(f(attn_sink)[0][None, :], (128, 8)))
    ins = []
    for r in range(NC):
        b, q = divmod(r, 4)
        grp = [P[4 * b + j] for j in range(4)]
        s0 = q * 2048
        qa = P[r]["qa_o"].reshape(2, 4, 128, 16, 128).transpose(2, 0, 3, 1, 4).reshape(128, 2, 16, 512)

        def ext(name, g):
            full = np.concatenate([gp[name][g][:, :2048] for gp in grp], axis=1)
            cx_ = np.concatenate([gp[name][g][:, 2048:] for gp in grp], axis=1)
            left = full[:, s0 - 128:s0] if q > 0 else np.zeros((128, 128), full.dtype)
            right = full[:, s0 + 2048:s0 + 2176] if q < 3 else np.zeros((128, 128), full.dtype)
            return np.concatenate([left, full[:, s0:s0 + 2048], right, cx_], axis=1)
        kae = np.stack([ext("ka_o", g) for g in range(2)], axis=1)
        vae = np.stack([ext("va_o", g).reshape(128, 20, 128).transpose(2, 1, 0) for g in range(2)], axis=1)
        masks = np.stack([m_prev, m_next, zero if q == 0 else m_prev, zero if q == 3 else m_next], axis=1).astype(bdt)
        ckvn = np.stack([np.concatenate([gp["ckvn_o"][kc][:, 2048:] for gp in grp] + [gp["ckvn_o"][kc][:, :2048] for gp in grp], axis=1)
                         for kc in range(2)], axis=1)
        kr = np.concatenate([gp["kr_o"][:, 2048:] for gp in grp] + [gp["kr_o"][:, :2048] for gp in grp], axis=1)
        ins.append(dict(qa=np.ascontiguousarray(qa), kae=np.ascontiguousarray(kae), vae=np.ascontiguousarray(vae),
                        masks=np.ascontiguousarray(masks), sink=sink_l, ckvn=np.ascontiguousarray(ckvn),
                        kr=np.ascontiguousarray(kr), qn=P[r]["qn_o"], qr=P[r]["qr_o"], wk=wk_l, wv=wv_l))
    A = run(build_attn(), ins)
    ocat = [np.asarray(A[r]["ocat"]) for r in range(NC)]
    del ins, A, P
    wgu_a = lay_wgu(f(ffn2_w_gate)[0], f(ffn2_w_up)[0])
    wd_a = lay_wd(f(ffn2_w_down)[0])
    wgu_b = lay_wgu(f(ffn1_w_gate)[1], f(ffn1_w_up)[1])
    wd_b = lay_wd(f(ffn1_w_down)[1])
    w_out_l = lay_lin(f(attn_w_out)[0])
    ins = []
    for r in range(NC):
        b, q = divmod(r, 4)
        ins.append(dict(ocat=ocat[r], xt=np.ascontiguousarray(xt1[r][:, :2048]), w_out=w_out_l, mods0=mx[0][b], mods1=mx[1][b],
                        n_f2=lay_vec(norm_ffn2[0]), n_f1=lay_vec(norm_ffn1[1]), n_mx=lay_vec(norm_mix[1]),
                        wgu_a=wgu_a, wd_a=wd_a, wgu_b=wgu_b, wd_b=wd_b))
    R4 = run(build_post(), ins)
    xt4 = [np.asarray(R4[r]["xt_out"]) for r in range(NC)]
    h1 = [np.asarray(R4[r]["h1_out"]) for r in range(NC)]
    del ins, R4, wgu_a, wd_a, wgu_b, wd_b
    wgu_c = lay_wgu(f(ffn2_w_gate)[1], f(ffn2_w_up)[1])
    wd_c = lay_wd(f(ffn2_w_down)[1])
    pw_l = np.concatenate([lay_lin(f(pool_w)[0][g]) for g in range(4)], axis=0)
    ins = []
    for r in range(NC):
        b, q = divmod(r, 4)
        s0 = q * 2048
        H = np.concatenate([h1[4 * b + j] for j in range(4)], axis=1)
        left = H[:, s0 - HALO_L:s0] if q > 0 else np.zeros((2048, HALO_L), np.float32)
        right = H[:, s0 + 2048:s0 + 2048 + HALO_R] if q < 3 else np.zeros((2048, HALO_R), np.float32)
        h1e = np.concatenate([left, H[:, s0:s0 + 2048], right], axis=1)
        t = s0 + np.arange(2048)
        ic = []
        for w in (2, 4, 8, 16):
            lo = np.clip(t - w // 2, 0, 8192)
            hi = np.clip(t - w // 2 + w, 0, 8192)
            ic.append((1.0 / (hi - lo).astype(np.float32)).astype(np.float32))
        invc = np.ascontiguousarray(np.broadcast_to(np.stack(ic)[None], (128, 4, 2048)))
        ins.append(dict(xt=xt4[r], h1e=np.ascontiguousarray(h1e), invc=invc, pw=pw_l, pscale=lay_vec(f(pool_scale)[0]),
                        mods1=mx[1][b], n_f2=lay_vec(norm_ffn2[1]), n_fin=lay_vec(f(final_norm)), wgu=wgu_c, wd=wd_c))
    R5 = run(build_pool(), ins)
    out = np.empty((2, 8192, 2048), np.float32)
    for r in range(NC):
        b, q = divmod(r, 4)
        out[b, q * 2048:(q + 1) * 2048] = np.asarray(R5[r]["outT"]).T
    return out
```
